# Optimizing a Trainium2 kernel written in Bass

```python
import jax, jax.numpy as jnp
from jax import lax
import numpy as np

D_MODEL = 1024
BATCH = 8
SEQ = 4096
DEPTH = 2

W_BRANCH = D_MODEL
N_BRANCH = 3
POOL_WINDOWS = (2, 4, 8, 16)
N_POOL_GROUPS = len(POOL_WINDOWS)
POOL_GROUP = W_BRANCH // N_POOL_GROUPS
SHORT_CONV = 3
CONFORMER_CONV = 31
LN_EPS = 1e-5
ALPHA = (2 * DEPTH) ** 0.25
BETA = (8 * DEPTH) ** -0.25

SPLIT_SIZES = (
    W_BRANCH, W_BRANCH,
    W_BRANCH, W_BRANCH, W_BRANCH, W_BRANCH,
    W_BRANCH, W_BRANCH, W_BRANCH,
    N_BRANCH * D_MODEL,
)
IN_COLS = int(sum(SPLIT_SIZES))
SPLIT_IDX = [int(v) for v in np.cumsum(SPLIT_SIZES)[:-1]]

kernel_name = "gated_parallel_pool_shortconv_conformer_trunk"


def layer_norm(x, g=None, b=None):
    x32 = x.astype(jnp.float32)
    mu = jnp.mean(x32, axis=-1, keepdims=True)
    var = jnp.mean(jnp.square(x32 - mu), axis=-1, keepdims=True)
    y = (x32 - mu) * lax.rsqrt(var + LN_EPS)
    if g is not None:
        y = y * g.astype(jnp.float32) + b.astype(jnp.float32)
    return y.astype(x.dtype)


def causal_dwconv(u, w, b):
    K, C = w.shape
    y = lax.conv_general_dilated(
        u, w[:, None, :].astype(u.dtype), window_strides=(1,),
        padding=[(K - 1, 0)], dimension_numbers=("NWC", "WIO", "NWC"),
        feature_group_count=C)
    return y + b


def causal_multiscale_pool(v):
    bsz, slen, _ = v.shape
    vg = v.reshape(bsz, slen, N_POOL_GROUPS, POOL_GROUP).astype(jnp.float32)
    cs = jnp.cumsum(vg, axis=1)
    t = jnp.arange(1, slen + 1, dtype=jnp.float32)
    outs = []
    for g, w in enumerate(POOL_WINDOWS):
        c_g = cs[:, :, g]
        lagged = jnp.pad(c_g, ((0, 0), (w, 0), (0, 0)))[:, :slen]
        cnt = jnp.minimum(t, float(w))[None, :, None]
        outs.append((c_g - lagged) / cnt)
    pooled = jnp.stack(outs, axis=2)
    return (pooled - vg).astype(v.dtype)


def hybrid_layer(x, c_act, w_ada, b_ada, w_in, b_in, pool_w, pool_scale, sc_w, sc_b,
                 cf_w, cf_b, cf_ln_g, cf_ln_b, w_branch, w_out, ln_g, ln_b):
    bsz, slen, _ = x.shape
    mod = (c_act @ w_ada + b_ada)[:, None, :]
    shift, scale, gate = jnp.split(mod, 3, axis=-1)
    u = layer_norm(x) * (1 + scale) + shift
    proj = u @ w_in + b_in
    a_v, a_g, b_b, b_c, b_h, b_g, c_a, c_b, c_g, merge = jnp.split(proj, SPLIT_IDX, axis=-1)

    pooled = causal_multiscale_pool(a_v)
    y_a = jnp.einsum("bsgi,gio->bsgo", pooled, pool_w).reshape(bsz, slen, W_BRANCH)
    y_a = y_a * pool_scale * jax.nn.silu(a_g)

    y_b = b_b * causal_dwconv(b_c * b_h, sc_w, sc_b) * jax.nn.silu(b_g)

    h_c = causal_dwconv(c_a * jax.nn.sigmoid(c_b), cf_w, cf_b)
    y_c = jax.nn.silu(layer_norm(h_c, cf_ln_g, cf_ln_b)) * jax.nn.silu(c_g)

    g_m = jax.nn.sigmoid(merge).reshape(bsz, slen, N_BRANCH, D_MODEL)
    m = (g_m[:, :, 0] * (y_a @ w_branch[0])
         + g_m[:, :, 1] * (y_b @ w_branch[1])
         + g_m[:, :, 2] * (y_c @ w_branch[2]))
    out = m @ w_out
    return layer_norm(ALPHA * x + gate * out, ln_g, ln_b)


def setup_inputs(seed: int = 0) -> dict:
    key = jax.random.key(seed)
    ks = jax.random.split(key, 20)
    L, D, W = DEPTH, D_MODEL, W_BRANCH
    n = lambda k, shp, s: jax.random.normal(k, shp, jnp.float32) * s
    return {
        "x": n(ks[0], (BATCH, SEQ, D), 1.0),
        "c": n(ks[1], (BATCH, D), 1.0),
        "w_ada": n(ks[2], (L, D, 3 * D), 0.5 * D ** -0.5),
        "b_ada": n(ks[3], (L, 3 * D), 0.02),
        "w_in": n(ks[4], (L, D, IN_COLS), D ** -0.5),
        "b_in": n(ks[5], (L, IN_COLS), 0.02),
        "pool_w": n(ks[6], (L, N_POOL_GROUPS, POOL_GROUP, POOL_GROUP), POOL_GROUP ** -0.5),
        "pool_scale": 1.0 + n(ks[7], (L, W), 0.1),
        "sc_w": n(ks[8], (L, SHORT_CONV, W), SHORT_CONV ** -0.5),
        "sc_b": n(ks[9], (L, W), 0.02),
        "cf_w": n(ks[10], (L, CONFORMER_CONV, W), CONFORMER_CONV ** -0.5),
        "cf_b": n(ks[11], (L, W), 0.02),
        "cf_ln_g": 1.0 + n(ks[12], (L, W), 0.05),
        "cf_ln_b": n(ks[13], (L, W), 0.02),
        "w_branch": n(ks[14], (L, N_BRANCH, W, D), BETA * W ** -0.5),
        "w_out": n(ks[15], (L, D, D), BETA * D ** -0.5),
        "ln_g": 1.0 + n(ks[16], (L, D), 0.05),
        "ln_b": n(ks[17], (L, D), 0.02),
    }


def reference(x, c, w_ada, b_ada, w_in, b_in, pool_w, pool_scale, sc_w, sc_b,
              cf_w, cf_b, cf_ln_g, cf_ln_b, w_branch, w_out, ln_g, ln_b):
    c_act = jax.nn.silu(c)
    for l in range(DEPTH):
        x = hybrid_layer(x, c_act, w_ada[l], b_ada[l], w_in[l], b_in[l], pool_w[l],
                         pool_scale[l], sc_w[l], sc_b[l], cf_w[l], cf_b[l],
                         cf_ln_g[l], cf_ln_b[l], w_branch[l], w_out[l], ln_g[l], ln_b[l])
    return x
```

```python
import numpy as np
from contextlib import ExitStack
import concourse.bass as bass
import concourse.mybir as mybir
from concourse.bass_utils import run_bass_kernel_spmd

F32 = mybir.dt.float32
BF16 = mybir.dt.bfloat16
AF = mybir.ActivationFunctionType
ALU = mybir.AluOpType

D = 1024
SEQ = 4096
TB = 512
NT = SEQ // TB
DEPTH = 2
ALPHA = (2 * DEPTH) ** 0.25
LN_EPS = 1e-5
WINS = (2, 4, 8, 16)
NV = 448
C_BIN, C_BADA, C_PSC, C_SCB, C_CFB, C_CLG, C_CLB, C_LNG, C_LNB, C_SCW, C_CFW = (
    0, 96, 120, 128, 136, 144, 152, 160, 168, 176, 200)


class Sched:
    ENG = ("pe", "act", "dve", "pool", "sp")

    def __init__(self):
        self.ops = {e: [] for e in self.ENG}
        self.cnt = {}
        self.domains = []
        self.waited = {e: {} for e in self.ENG}
        self.res = {}

    def _dom(self, d):
        if d not in self.cnt:
            self.cnt[d] = 0
            self.domains.append(d)

    def op(self, eng, fn, reads=(), writes=(), inc=True, dma=None):
        if dma is not None:
            dom, step = "dma:" + dma, 16
        else:
            dom, step = eng, 1
        self._dom(dom)
        waits = {}

        def need(d, v, war=False):
            if dma is None and d == eng:
                if eng == "pe":
                    return
                if war and eng in ("act", "dve"):
                    return
            if self.waited[eng].get(d, 0) >= v:
                return
            assert v <= self.cnt[d], ("wait on future increment", eng, d, v, self.cnt[d])
            waits[d] = max(waits.get(d, 0), v)

        for k in reads:
            r = self.res.get(k)
            if r and r[0]:
                need(*r[0])
        for k in writes:
            r = self.res.get(k)
            if r:
                if r[0]:
                    need(*r[0])
                for d, v in r[1].items():
                    need(d, v, war=True)
        for d, v in waits.items():
            self.waited[eng][d] = v
        counted = inc or dma is not None
        if counted:
            self.cnt[dom] += step
            val = self.cnt[dom]
        else:
            val = self.cnt[dom] + step
        for k in reads:
            r = self.res.setdefault(k, [None, {}])
            r[1][dom] = max(r[1].get(dom, 0), val)
        for k in writes:
            self.res[k] = [(dom, val), {}]
        self.ops[eng].append((list(waits.items()), fn, dom if counted else None, step))

    def wait_all(self, eng, keys):
        waits = {}
        for k in keys:
            r = self.res.get(k)
            if not r:
                continue
            items = list(r[1].items())
            if r[0]:
                items.append(r[0])
            for d, v in items:
                if self.waited[eng].get(d, 0) < v:
                    waits[d] = max(waits.get(d, 0), v)
        for d, v in waits.items():
            self.waited[eng][d] = v
        self.ops[eng].append((list(waits.items()), None, None, 0))

    def emit(self, nc, sems):
        def run(name):
            def body(e):
                for waits, fn, dom, step in self.ops[name]:
                    for d, v in waits:
                        e.wait_ge(sems[d], v)
                    if fn is None:
                        continue
                    ins = fn(e)
                    if dom is not None:
                        ins.then_inc(sems[dom], step)
            return body

        with nc.Block() as block:
            block.tensor(run("pe"))
            block.scalar(run("act"))
            block.vector(run("dve"))
            block.gpsimd(run("pool"))
            block.sync(run("sp"))


def build_program(tiles=tuple(range(NT)), layers=(0, 1), nseq=SEQ):
    nc = bass.Bass("TRN2", target_bir_lowering=False)
    x_d = nc.dram_tensor("x", [nseq, D], F32, kind="ExternalInput").ap()
    out_d = nc.dram_tensor("out", [nseq, D], F32, kind="ExternalOutput").ap()
    ws_d = nc.dram_tensor("ws", [DEPTH, 128, 128, 1024], F32, kind="ExternalInput").ap()
    wada_d = nc.dram_tensor("wada", [DEPTH, 24, 128, 1024], F32, kind="ExternalInput").ap()
    pw_d = nc.dram_tensor("pw", [DEPTH, 128, 2048], F32, kind="ExternalInput").ap()
    cv_d = nc.dram_tensor("cv", [DEPTH, 128, NV], F32, kind="ExternalInput").ap()
    cc_d = nc.dram_tensor("ccol", [128, 8], F32, kind="ExternalInput").ap()
    id_d = nc.dram_tensor("ident", [128, 128], F32, kind="ExternalInput").ap()
    pm_d = nc.dram_tensor("pmat", [128, 12, 128], F32, kind="ExternalInput").ap()

    S = Sched()
    plan = []
    with ExitStack() as es:
        def sb(name, shape, dt):
            return es.enter_context(nc.sbuf_tensor(name, shape, dt))

        x_fm = sb("x_fm", [128, 8, TB], F32)
        mxs = sb("mxs", [128, 8, TB], F32)
        xs = mxs[:].rearrange("p a b -> p (a b)").rearrange("p (s c) -> p s c", s=4)
        u = sb("u", [128, 8, TB], BF16)
        v_tok = sb("v_tok", [128, 5, D], BF16)
        vhalo = sb("vhalo", [128, DEPTH, D], BF16)
        sa = sb("sa", [128, 4, TB], BF16)
        y = sb("y", [128, 8, TB], BF16)
        hm = sb("hm", [128, 8, TB], BF16)
        g_t = sb("g_t", [128, 2, TB + 30], BF16)
        hal_g = sb("hal_g", [128, DEPTH, 8, 30], BF16)
        q_t = sb("q_t", [128, 2, TB + 2], BF16)
        hal_q = sb("hal_q", [128, DEPTH, 8, 2], BF16)
        accr = sb("accr", [128, 2, TB], F32)
        hsq = sb("hsq", [128, 2, TB], BF16)
        NSTG, NWB, NTB, NTF = 2, 5, 8, 6
        stage = sb("stage", [128, NSTG, 2048], F32)
        wb = sb("wb", [128, NWB, 2048], BF16)
        pwb = sb("pwb", [128, DEPTH, 2048], BF16)
        tb_ = sb("tmpb", [128, NTB, TB], BF16)
        tf_ = sb("tmpf", [128, NTF, TB], F32)
        st_ = sb("stat", [128, 5, TB], F32)
        ident = sb("identf", [128, 128], F32)
        pmf = sb("pmf", [128, 12, 128], F32)
        pmb = sb("pmb", [128, 12, 128], BF16)
        ones = sb("ones", [128, 128], BF16)
        cv = sb("cvs", [128, DEPTH, NV], F32)
        hb = sb("hb", [128, DEPTH, 96], F32)
        cfwh = sb("cfwh", [128, DEPTH, 248], F32)
        modc = sb("modc", [128, DEPTH, 24], F32)
        sc1 = sb("sc1", [128, DEPTH, 8], F32)
        gth = sb("gth", [128, DEPTH, 8], F32)
        epsc = sb("epsc", [128, 2], F32)
        ccol = sb("ccols", [128, 8], F32)
        sc2 = sb("sc2", [128, 8, 2], F32)
        banks = [es.enter_context(nc.psum_tensor(f"bk{i}", [128, TB], F32)) for i in range(8)]

        def act(out, in_, func, bias=0.0, scale=1.0, r=(), w=()):
            S.op("act", lambda e: e.activation(out=out, in_=in_, func=func, bias=bias, scale=scale), reads=r, writes=w)

        def tt(out, in0, in1, op, r=(), w=()):
            S.op("dve", lambda e: e.tensor_tensor(out=out, in0=in0, in1=in1, op=op), reads=r, writes=w)

        def stt(out, in0, scalar, in1, op0, op1, r=(), w=()):
            S.op("dve", lambda e: e.scalar_tensor_tensor(out=out, in0=in0, scalar=scalar, in1=in1, op0=op0, op1=op1), reads=r, writes=w)

        def tsmul(out, in0, s1, r=(), w=()):
            S.op("dve", lambda e: e.tensor_scalar_mul(out=out, in0=in0, scalar1=s1), reads=r, writes=w)

        def tsadd(out, in0, s1, r=(), w=()):
            S.op("dve", lambda e: e.tensor_scalar_add(out=out, in0=in0, scalar1=s1), reads=r, writes=w)

        def cp(out, in_, r=(), w=()):
            S.op("dve", lambda e: e.tensor_copy(out=out, in_=in_), reads=r, writes=w)

        def mm(out, lhsT, rhs, start, stop, r=(), w=(), inc=None):
            S.op("pe", lambda e: e.matmul(out, lhsT=lhsT, rhs=rhs, start=start, stop=stop), reads=r, writes=w,
                 inc=stop if inc is None else inc)

        def tr(out, in_, r=(), w=(), inc=True):
            S.op("pe", lambda e: e.transpose(out, in_, ident[:]), reads=tuple(r) + ("ident",), writes=w, inc=inc)

        def dma(eng, out, in_, chan, r=(), w=()):
            S.op(eng, lambda e: e.dma_start(out=out, in_=in_), reads=r, writes=w, dma=chan)

        st = {"bank": 0, "tb": 0, "tf": 0, "acc": 0, "gq": 0, "stg": 0, "wbn": 0, "issued": 0, "sa": 0, "qq": 0}

        def bank(lo=0, n=6):
            i = lo + st["bank"] % n
            st["bank"] += 1
            return banks[i], ("bk", i)

        def tmpb():
            i = st["tb"] % NTB
            st["tb"] += 1
            return tb_[:, i, :], ("tb", i)

        def tmpf():
            i = st["tf"] % NTF
            st["tf"] += 1
            return tf_[:, i, :], ("tf", i)

        def col(l, c):
            return cv[:, l, c:c + 1]

        seq = []
        def issue(n):
            l, pp = seq[n]
            si = st["stg"] % NSTG
            st["stg"] += 1
            wi = n % NWB
            dma("sp", stage[:, si, :].rearrange("p (c f) -> p c f", c=2),
                ws_d[l, 2 * pp:2 * pp + 2].rearrange("c p f -> p c f"), f"stg{si}", w=[("stg", si)])
            act(wb[:, wi, :], stage[:, si, :], AF.Copy, r=[("stg", si)], w=[("wb", wi)])

        def getw(d0, d1, tl_first):
            n = st["wbn"]
            st["wbn"] += 1
            if tl_first:
                plan.append(d0)
                plan.append(d1)
            else:
                k = (n % 64) * 2
                assert plan[k] == d0 and plan[k + 1] == d1, (plan[k], d0, plan[k + 1], d1)
            while st["issued"] <= min(n + NWB - 2, len(seq) - 1):
                issue(st["issued"])
                st["issued"] += 1
            wi = n % NWB
            return wb[:, wi, :].rearrange("p (c k m) -> p c k m", c=2, k=8), ("wb", wi)

        def proj(wv, j, wkey, extra_r=()):
            bk, bkey = bank()
            for k in range(8):
                mm(bk[:], wv[:, j, k, :], u[:, k, :], k == 0, k == 7, r=[wkey, ("u", k)] + list(extra_r), w=[bkey])
            return bk, bkey

        MEANB, MSQB = banks[6], banks[7]

        def stats_mm(xb_ap, xsq_ap, first, last, r):
            mm(MEANB[:], ones[:], xb_ap, first, last, r=list(r) + ["ones"], w=[("bk", 6)], inc=True)
            mm(MSQB[:], ones[:], xsq_ap, first, last, r=list(r) + ["ones"], w=[("bk", 7)], inc=True)

        def stats_fin(eps_idx):
            mean_sb, m2, var, rstd, nmr = (st_[:, i, :] for i in range(5))
            act(mean_sb, MEANB[:], AF.Copy, r=[("bk", 6)], w=["st0"])
            tt(m2, mean_sb, mean_sb, ALU.mult, r=["st0"], w=["st1"])
            tt(var, MSQB[:], m2, ALU.subtract, r=[("bk", 7), "st1"], w=["st2"])
            act(var, var, AF.Sqrt, bias=epsc[:, eps_idx:eps_idx + 1], r=["st2", "epsc"], w=["st2"])
            S.op("dve", lambda e: e.reciprocal(out=rstd, in_=var), reads=["st2"], writes=["st3"])
            stt(nmr, mean_sb, -1.0, rstd, ALU.mult, ALU.mult, r=["st0", "st3"], w=["st4"])
            return rstd, nmr

        def ln_stats_x():
            for c in range(8):
                xb, kb = tmpb()
                xq, kq = tmpb()
                act(xb, x_fm[:, c, :], AF.Copy, r=[("x", c)], w=[kb])
                act(xq, x_fm[:, c, :], AF.Square, r=[("x", c)], w=[kq])
                stats_mm(xb, xq, c == 0, c == 7, [kb, kq])

        def ln_apply_x(rstd, nmr, out_fn):
            for c in range(8):
                t, kt = tmpf()
                tt(t, x_fm[:, c, :], rstd, ALU.mult, r=[("x", c), "st3"], w=[kt])
                tt(t, t, nmr, ALU.add, r=[kt, "st4"], w=[kt])
                out_fn(c, t, kt)

        dma("sp", ident[:], id_d, "c0", w=["ident"])
        dma("sp", pmf[:], pm_d, "c1", w=["pmf"])
        dma("sp", cv[:], cv_d.rearrange("l p n -> p l n"), "c2", w=["cv"])
        dma("sp", ccol[:], cc_d, "c3", w=["ccol"])
        cp(pmb[:], pmf[:], r=["pmf"], w=["pmb"])
        S.op("dve", lambda e: e.memset(ones[:], 1.0 / 1024.0), writes=["ones"])
        S.op("dve", lambda e: e.memset(vhalo[:], 0.0), writes=[("vh", 0), ("vh", 1)])
        S.op("dve", lambda e: e.memset(hal_g[:], 0.0), writes=["halg"])
        S.op("dve", lambda e: e.memset(hal_q[:], 0.0), writes=["halq"])
        S.op("dve", lambda e: e.memset(epsc[:, 0:1], LN_EPS), writes=["epsc0"])
        S.op("dve", lambda e: e.memset(epsc[:, 1:2], LN_EPS / (ALPHA * ALPHA)), reads=["epsc0"], writes=["epsc"])
        for l in range(DEPTH):
            tsmul(hb[:, l, :], cv[:, l, C_BIN:C_BIN + 96], 0.5, r=["cv"], w=[("hb", l)])
            tsmul(cfwh[:, l, :], cv[:, l, C_CFW:C_CFW + 248], 0.5, r=["cv"], w=[("cfwh", l)])
        for j in range(2):
            act(sc2[:, :, j], ccol[:], AF.Silu, r=["ccol"], w=[("sc2", j)])
        for l in range(DEPTH):
            si = st["stg"] % NSTG
            st["stg"] += 1
            dma("sp", stage[:, si, :], pw_d[l], f"stg{si}", w=[("stg", si)])
            act(pwb[:, l, :], stage[:, si, :], AF.Copy, r=[("stg", si)], w=[("pwb", l)])
        for l in layers:
            mbk, mkey = bank()
            for qp in range(12):
                si = st["stg"] % NSTG
                st["stg"] += 1
                dma("sp", stage[:, si, :].rearrange("p (c f) -> p c f", c=2),
                    wada_d[l, 2 * qp:2 * qp + 2].rearrange("c p f -> p c f"), f"stg{si}", w=[("stg", si)])
                sv = stage[:, si, :].rearrange("p (c k m) -> p c k m", c=2, k=8)
                for j in range(2):
                    q = 2 * qp + j
                    for k in range(8):
                        mm(mbk[:, 2 * q:2 * q + 2], sv[:, j, k, :], sc2[:, k, :], k == 0, k == 7,
                           r=[("stg", si), ("sc2", 0), ("sc2", 1)], w=[mkey], inc=(k == 7))
            mv = mbk[:, 0:48].rearrange("p (q t) -> p q t", t=2)[:, :, 0]
            tt(modc[:, l, :], mv, cv[:, l, C_BADA:C_BADA + 24], ALU.add, r=[mkey, "cv"], w=[("modc", l)])
            tsadd(sc1[:, l, :], modc[:, l, 8:16], 1.0, r=[("modc", l)], w=[("sc1", l)])
            tsmul(gth[:, l, :], modc[:, l, 16:24], 0.5 / ALPHA, r=[("modc", l)], w=[("gth", l)])

        for _ in tiles:
            for l in layers:
                for pp in range(64):
                    seq.append((l, pp))

        def tile_layer(ti, l, first_tl):
            first_tile = (ti == tiles[0])
            G = lambda d0, d1: getw(d0, d1, first_tl)
            pend_stats = []

            ln_stats_x()
            rstd, nmr = stats_fin(0)

            def to_u(c, t, kt):
                act(u[:, c, :], t, AF.Identity, bias=modc[:, l, c:c + 1], scale=sc1[:, l, c:c + 1],
                    r=[kt, ("modc", l), ("sc1", l)], w=[("u", c)])
            ln_apply_x(rstd, nmr, to_u)

            def c1(c):
                wv, wk = G(("win", 48 + c), ("win", 56 + c))
                bka, ka = proj(wv, 0, wk)
                bkb, kb_ = proj(wv, 1, wk)
                th, kth = tmpb()
                ab, kab = tmpb()
                act(th, bkb[:], AF.Tanh, bias=hb[:, l, 56 + c:57 + c], scale=0.5, r=[kb_, ("hb", l)], w=[kth])
                act(ab, bka[:], AF.Identity, bias=col(l, C_BIN + 48 + c), r=[ka, "cv"], w=[kab])
                gi = st["gq"] % 2
                st["gq"] += 1
                gt, kg = g_t[:, gi, :], ("gt", gi)
                cp(gt[:, 0:30], hal_g[:, l, c, :], r=["halg"], w=[kg])
                stt(gt[:, 30:30 + TB], th, 1.0, ab, ALU.add, ALU.mult, r=[kth, kab, kg], w=[kg])
                cp(hal_g[:, l, c, :], gt[:, TB:TB + 30], r=[kg], w=["halg"])
                ai = st["acc"] % 2
                st["acc"] += 1
                acc, kacc = accr[:, ai, :], ("acc", ai)
                act(acc, gt[:, 30:30 + TB], AF.Identity, bias=col(l, C_CFB + c), scale=cfwh[:, l, c * 31 + 30:c * 31 + 31],
                    r=[kg, "cv", ("cfwh", l)], w=[kacc])
                for k in range(30):
                    stt(acc, gt[:, k:k + TB], cfwh[:, l, c * 31 + k:c * 31 + k + 1], acc, ALU.mult, ALU.add,
                        r=[kg, kacc, ("cfwh", l)], w=[kacc])
                hq, khq = hsq[:, c % 2, :], ("hsq", c % 2)
                act(hm[:, c, :], acc, AF.Copy, r=[kacc], w=[("hm", c)])
                act(hq, acc, AF.Square, r=[kacc], w=[khq])
                pend_stats.append((c, hq, khq))

            def flush_stats():
                while pend_stats:
                    c, hq, khq = pend_stats.pop(0)
                    stats_mm(hm[:, c, :], hq, c == 0, c == 7, [("hm", c), khq])

            def a_av():
                cp(v_tok[:, 0, :], vhalo[:, l, :], r=[("vh", l)], w=[("v", 0)])
                for pr in range(4):
                    wv, wk = G(("win", 2 * pr), ("win", 2 * pr + 1))
                    for sh in range(2):
                        bk, bkey = bank()
                        for s in (2 * sh, 2 * sh + 1):
                            for k in range(8):
                                mm(bk[:, (s % 2) * 256:(s % 2) * 256 + 256], u[:, k, s * 128:(s + 1) * 128],
                                   wv[:, :, k, :], k == 0, k == 7, r=[wk, ("u", k)], w=[bkey],
                                   inc=(k == 7 and s % 2 == 1))
                        cp(v_tok[:, 1 + 2 * sh:3 + 2 * sh, pr * 256:(pr + 1) * 256],
                           bk[:].rearrange("p (s c) -> p s c", s=2), r=[bkey],
                           w=[("v", 1 + 2 * sh), ("v", 2 + 2 * sh)])
                cp(vhalo[:, l, :], v_tok[:, 4, :], r=[("v", 4)], w=[("vh", l)])

            def a_g(g):
                wv, wk = G(("win", 8 + 2 * g), ("win", 9 + 2 * g))
                sas = []
                for j in range(2):
                    bk, bkey = proj(wv, j, wk)
                    si_ = st["sa"] % 4
                    st["sa"] += 1
                    act(sa[:, si_, :], bk[:], AF.Silu, bias=col(l, C_BIN + 8 + 2 * g + j), r=[bkey, "cv"], w=[("sa", si_)])
                    sas.append(si_)
                pooled = []
                for j in range(2):
                    cc = 2 * g + j
                    bk, bkey = bank()
                    for s in range(4):
                        dm = pmb[:, 8 + g, :] if (first_tile and s == 0) else pmb[:, g, :]
                        mm(bk[:, s * 128:(s + 1) * 128], v_tok[:, s, cc * 128:(cc + 1) * 128], pmb[:, 4 + g, :],
                           True, False, r=[("v", s), "pmb"], w=[bkey], inc=False)
                        mm(bk[:, s * 128:(s + 1) * 128], v_tok[:, s + 1, cc * 128:(cc + 1) * 128], dm,
                           False, True, r=[("v", s + 1), "pmb"], w=[bkey], inc=(s == 3))
                    pl, kpl = tmpb()
                    cp(pl, bk[:], r=[bkey], w=[kpl])
                    pooled.append((pl, kpl))
                pwv = pwb[:, l, :].rearrange("p (g o k m) -> p g o k m", g=4, o=2, k=2)
                for oc in range(2):
                    bk, bkey = bank()
                    for kc in range(2):
                        mm(bk[:], pwv[:, g, oc, kc, :], pooled[kc][0], kc == 0, kc == 1,
                           r=[("pwb", l), pooled[kc][1]], w=[bkey])
                    stt(y[:, 2 * g + oc, :], bk[:], col(l, C_PSC + 2 * g + oc), sa[:, sas[oc], :], ALU.mult, ALU.mult,
                        r=[bkey, "cv", ("sa", sas[oc])], w=[("y", 2 * g + oc)])

            def z_branch(br, ocps):
                for ocp in ocps:
                    wp, kp = G(("wbr", br, 2 * ocp), ("wbr", br, 2 * ocp + 1))
                    wg, kgt = G(("win", 72 + 8 * br + 2 * ocp), ("win", 73 + 8 * br + 2 * ocp))
                    for j in range(2):
                        oc = 2 * ocp + j
                        bz, kz = bank()
                        for k in range(8):
                            mm(bz[:], wp[:, j, k, :], y[:, k, :], k == 0, k == 7, r=[kp, ("y", k)], w=[kz])
                        bg, kbg = proj(wg, j, kgt)
                        th, kth = tmpb()
                        gcol = 72 + 8 * br + oc
                        act(th, bg[:], AF.Tanh, bias=hb[:, l, gcol:gcol + 1], scale=0.5, r=[kbg, ("hb", l)], w=[kth])
                        if br == 0:
                            stt(mxs[:, oc, :], th, 1.0, bz[:], ALU.add, ALU.mult, r=[kth, kz], w=[("mxs", oc)])
                        else:
                            t, kt = tmpf()
                            stt(t, th, 1.0, bz[:], ALU.add, ALU.mult, r=[kth, kz], w=[kt])
                            if br == 1:
                                tt(mxs[:, oc, :], mxs[:, oc, :], t, ALU.add, r=[("mxs", oc), kt], w=[("mxs", oc)])
                            else:
                                tt(hm[:, oc, :], mxs[:, oc, :], t, ALU.add, r=[("mxs", oc), kt], w=[("hm", oc)])

            def b_chunk(c):
                w1, k1 = G(("win", 24 + c), ("win", 32 + c))
                bc_, kbc = proj(w1, 0, k1)
                bh_, kbh = proj(w1, 1, k1)
                bcs, kbcs = tmpf()
                act(bcs, bc_[:], AF.Identity, bias=col(l, C_BIN + 24 + c), r=[kbc, "cv"], w=[kbcs])
                qi = st["qq"] % 2
                st["qq"] += 1
                qt, kq = q_t[:, qi, :], ("qt", qi)
                cp(qt[:, 0:2], hal_q[:, l, c, :], r=["halq"], w=[kq])
                stt(qt[:, 2:2 + TB], bh_[:], col(l, C_BIN + 32 + c), bcs, ALU.add, ALU.mult, r=[kbh, kbcs, "cv", kq], w=[kq])
                cp(hal_q[:, l, c, :], qt[:, TB:TB + 2], r=[kq], w=["halq"])
                a3, ka3 = tmpf()
                act(a3, qt[:, 2:2 + TB], AF.Identity, bias=col(l, C_SCB + c), scale=col(l, C_SCW + 3 * c + 2),
                    r=[kq, "cv"], w=[ka3])
                for k in range(2):
                    stt(a3, qt[:, k:k + TB], col(l, C_SCW + 3 * c + k), a3, ALU.mult, ALU.add, r=[kq, ka3, "cv"], w=[ka3])
                w2, k2 = G(("win", 16 + c), ("win", 40 + c))
                bb_, kbb = proj(w2, 0, k2)
                bg_, kbg = proj(w2, 1, k2)
                sg, ksg = tmpb()
                act(sg, bg_[:], AF.Silu, bias=col(l, C_BIN + 40 + c), r=[kbg, "cv"], w=[ksg])
                tB, ktB = tmpf()
                stt(tB, bb_[:], col(l, C_BIN + 16 + c), a3, ALU.add, ALU.mult, r=[kbb, ka3, "cv"], w=[ktB])
                tt(y[:, c, :], tB, sg, ALU.mult, r=[ktB, ksg], w=[("y", c)])

            others = [
                [a_av],
                [lambda: a_g(0), lambda: a_g(1)],
                [lambda: a_g(2), lambda: a_g(3)],
                [lambda: z_branch(0, range(4))],
                [lambda: b_chunk(0), lambda: b_chunk(1), lambda: b_chunk(2)],
                [lambda: b_chunk(3), lambda: b_chunk(4), lambda: b_chunk(5)],
                [lambda: b_chunk(6), lambda: b_chunk(7)],
                [lambda: z_branch(1, range(4))],
            ]
            for c in range(8):
                c1(c)
                for f in others[c]:
                    f()
                if c >= 1:
                    keep = pend_stats[-1:]
                    del pend_stats[-1:]
                    flush_stats()
                    pend_stats.extend(keep)
            flush_stats()
            rstd, nmr = stats_fin(0)
            for cpi in range(4):
                wv, wk = G(("win", 64 + 2 * cpi), ("win", 65 + 2 * cpi))
                for j in range(2):
                    c = 2 * cpi + j
                    bk, bkey = proj(wv, j, wk)
                    t, kt = tmpf()
                    tt(t, hm[:, c, :], rstd, ALU.mult, r=[("hm", c), "st3"], w=[kt])
                    tt(t, t, nmr, ALU.add, r=[kt, "st4"], w=[kt])
                    uc, kuc = tmpb()
                    act(uc, t, AF.Silu, bias=col(l, C_CLB + c), scale=col(l, C_CLG + c), r=[kt, "cv"], w=[kuc])
                    sg, ksg = tmpb()
                    act(sg, bk[:], AF.Silu, bias=col(l, C_BIN + 64 + c), r=[bkey, "cv"], w=[ksg])
                    tt(y[:, c, :], uc, sg, ALU.mult, r=[kuc, ksg], w=[("y", c)])
            z_branch(2, range(4))
            for ocp in range(4):
                wv, wk = G(("wout", 2 * ocp), ("wout", 2 * ocp + 1))
                for j in range(2):
                    oc = 2 * ocp + j
                    bk, bkey = bank()
                    for k in range(8):
                        mm(bk[:], wv[:, j, k, :], hm[:, k, :], k == 0, k == 7, r=[wk, ("hm", k)], w=[bkey])
                    stt(x_fm[:, oc, :], bk[:], gth[:, l, oc:oc + 1], x_fm[:, oc, :], ALU.mult, ALU.add,
                        r=[bkey, ("gth", l), ("x", oc)], w=[("x", oc)])
            ln_stats_x()
            rstd, nmr = stats_fin(1)

            def to_x(c, t, kt):
                act(x_fm[:, c, :], t, AF.Identity, bias=col(l, C_LNB + c), scale=col(l, C_LNG + c),
                    r=[kt, "cv"], w=[("x", c)])
            ln_apply_x(rstd, nmr, to_x)

        first_tl = True
        for ti in tiles:
            dma("sp", xs, x_d[ti * TB:(ti + 1) * TB, :].rearrange("(s p) c -> p s c", p=128), "xin",
                w=[("mxs", i) for i in range(8)])
            for c in range(8):
                bk, bkey = bank()
                for s in range(4):
                    tr(bk[:, s * 128:(s + 1) * 128], xs[:, s, c * 128:(c + 1) * 128],
                       r=[("mxs", 2 * s), ("mxs", 2 * s + 1)], w=[bkey], inc=(s == 3))
                cp(x_fm[:, c, :], bk[:], r=[bkey], w=[("x", c)])
            for l in layers:
                tile_layer(ti, l, first_tl)
                first_tl = False
            for s in range(4):
                for half in range(2):
                    bk, bkey = bank()
                    for cq in range(4):
                        c = half * 4 + cq
                        tr(bk[:, cq * 128:(cq + 1) * 128], x_fm[:, c, s * 128:(s + 1) * 128],
                           r=[("x", c)], w=[bkey], inc=(cq == 3))
                    cp(xs[:, s, half * 512:(half + 1) * 512], bk[:], r=[bkey], w=[("mxs", 2 * s + half)])
            dma("sp", out_d[ti * TB:(ti + 1) * TB, :].rearrange("(s p) c -> p s c", p=128), xs, "xout",
                r=[("mxs", i) for i in range(8)], w=["outd"])
        S.wait_all("sp", ["outd"])
        assert st["wbn"] == len(seq), (st["wbn"], len(seq))
        sems = {d: es.enter_context(nc.semaphore(d.replace(":", "_"))) for d in S.domains}
        S.emit(nc, sems)
    return nc, plan


def _pool_mats():
    pm = np.zeros((12, 128, 128), np.float32)
    for g, w in enumerate(WINS):
        for t in range(128):
            for s in range(max(0, t - w + 1), t + 1):
                pm[g, s, t] = 1.0 / w
                pm[8 + g, s, t] = 1.0 / min(t + 1, w)
            pm[g, t, t] -= 1.0
            pm[8 + g, t, t] -= 1.0
            for sp in range(128 + t - w + 1, 128):
                if sp >= 0:
                    pm[4 + g, sp, t] = 1.0 / w
    return np.ascontiguousarray(pm.transpose(1, 0, 2))


def _chunkT(w2d, cols):
    blk = w2d[:, cols]
    return blk.reshape(8, 128, 128).transpose(1, 0, 2).reshape(128, 1024)


_CACHE = {}


def kernel(x, c, w_ada, b_ada, w_in, b_in, pool_w, pool_scale, sc_w, sc_b,
           cf_w, cf_b, cf_ln_g, cf_ln_b, w_branch, w_out, ln_g, ln_b, _ntiles=NT, _ncores=8):
    key = ("nc", _ntiles)
    if key not in _CACHE:
        _CACHE[key] = build_program(tiles=tuple(range(_ntiles)), nseq=_ntiles * TB)
    nc, plan = _CACHE[key]
    f = lambda a: np.asarray(a, dtype=np.float32)
    x, c, w_ada, b_ada, w_in, b_in = f(x), f(c), f(w_ada), f(b_ada), f(w_in), f(b_in)
    pool_w, pool_scale, sc_w, sc_b, cf_w, cf_b = f(pool_w), f(pool_scale), f(sc_w), f(sc_b), f(cf_w), f(cf_b)
    cf_ln_g, cf_ln_b, w_branch, w_out, ln_g, ln_b = f(cf_ln_g), f(cf_ln_b), f(w_branch), f(w_out), f(ln_g), f(ln_b)
    ws = np.empty((DEPTH, 128, 128, 1024), np.float32)
    for l in range(DEPTH):
        for i, d in enumerate(plan):
            if d[0] == "win":
                ws[l, i] = _chunkT(w_in[l], slice(d[1] * 128, d[1] * 128 + 128))
            elif d[0] == "wbr":
                ws[l, i] = _chunkT(w_branch[l, d[1]], slice(d[2] * 128, d[2] * 128 + 128))
            else:
                ws[l, i] = _chunkT(w_out[l], slice(d[1] * 128, d[1] * 128 + 128))
    wada = np.empty((DEPTH, 24, 128, 1024), np.float32)
    for l in range(DEPTH):
        for q in range(24):
            wada[l, q] = _chunkT(w_ada[l], slice(q * 128, q * 128 + 128))
    pw = pool_w.reshape(DEPTH, 4, 2, 128, 2, 128).transpose(0, 3, 1, 4, 2, 5).reshape(DEPTH, 128, 2048)
    colv = lambda v: v.reshape(DEPTH, -1, 128).transpose(0, 2, 1)
    tapv = lambda v: v.reshape(DEPTH, v.shape[1], 8, 128).transpose(0, 3, 2, 1).reshape(DEPTH, 128, -1)
    cv = np.concatenate([colv(b_in), colv(b_ada), colv(pool_scale), colv(sc_b), colv(cf_b), colv(cf_ln_g),
                         colv(cf_ln_b), colv(ln_g), colv(ln_b), tapv(sc_w), tapv(cf_w)], axis=2)
    assert cv.shape == (DEPTH, 128, NV)
    common = {
        "ws": np.ascontiguousarray(ws), "wada": np.ascontiguousarray(wada), "pw": np.ascontiguousarray(pw),
        "cv": np.ascontiguousarray(cv), "ident": np.eye(128, dtype=np.float32), "pmat": _pool_mats(),
    }
    in_maps = []
    for b in range(_ncores):
        m = dict(common)
        m["x"] = np.ascontiguousarray(x[b, :_ntiles * TB])
        m["ccol"] = np.ascontiguousarray(c[b].reshape(8, 128).T)
        in_maps.append(m)
    res = run_bass_kernel_spmd(nc, in_maps, core_ids=list(range(_ncores)))
    return np.stack([np.asarray(r["out"], dtype=np.float32) for r in res.results], axis=0)
```
